# Optimizing a Trainium2 kernel written in Bass

```python
import math
import jax, jax.numpy as jnp
from jax import lax
import numpy as np

D_MODEL = 1024
BATCH = 8
SEQ = 8192
DEPTH = 1

HEAD_DIM = 64
N_ATTN_HEADS = 8
N_KV_HEADS = 2
N_GMLP_GROUPS = 8
GMLP_GROUP_DIM = 64
ATTN_WIDTH = N_ATTN_HEADS * HEAD_DIM
KV_WIDTH = N_KV_HEADS * HEAD_DIM
GMLP_WIDTH = N_GMLP_GROUPS * GMLP_GROUP_DIM
MIX_WIDTH = ATTN_WIDTH + GMLP_WIDTH
IN_WIDTH = ATTN_WIDTH + 2 * KV_WIDTH + 2 * GMLP_WIDTH
WINDOW = 128
BLOCK = 128
CHUNK = 128
N_BUCKETS = 32
MAX_DISTANCE = 128
D_FF = -(-8 * D_MODEL // (3 * 256)) * 256
ALPHA = (2 * DEPTH) ** 0.25
BETA = (8 * DEPTH) ** -0.25
LN_EPS = 1e-5
NEG_INF = -1e30

kernel_name = "hymba_gmlp_swa_sink_deepnorm_adaln"


def layer_norm(x, g, b):
    xf = x.astype(jnp.float32)
    mu = jnp.mean(xf, axis=-1, keepdims=True)
    var = jnp.mean(jnp.square(xf - mu), axis=-1, keepdims=True)
    return ((xf - mu) * lax.rsqrt(var + LN_EPS) * g.astype(jnp.float32) + b.astype(jnp.float32)).astype(x.dtype)


def rms_norm(x, g):
    xf = x.astype(jnp.float32)
    ms = jnp.mean(jnp.square(xf), axis=-1, keepdims=True)
    return (xf * lax.rsqrt(ms + LN_EPS) * g.astype(jnp.float32)).astype(x.dtype)


def t5_bucket(dist):
    max_exact = N_BUCKETS // 2
    n = jnp.maximum(dist, 0)
    nf = jnp.maximum(n, max_exact).astype(jnp.float32)
    large = max_exact + (jnp.log(nf / max_exact) / math.log(MAX_DISTANCE / max_exact)
                         * (N_BUCKETS - max_exact)).astype(jnp.int32)
    large = jnp.minimum(large, N_BUCKETS - 1)
    return jnp.where(n < max_exact, n, large)


def sliding_window_attention(q, k, v, sinks, rel_bias):
    B, S, H, Dh = q.shape
    nb = S // BLOCK
    G = H // N_KV_HEADS
    qb = q.reshape(B, nb, BLOCK, N_KV_HEADS, G, Dh)
    kb = k.reshape(B, nb, BLOCK, N_KV_HEADS, Dh)
    vb = v.reshape(B, nb, BLOCK, N_KV_HEADS, Dh)
    kpad = jnp.zeros_like(kb[:, :1])
    vpad = jnp.zeros_like(vb[:, :1])
    kk = jnp.concatenate([jnp.concatenate([kpad, kb[:, :-1]], axis=1), kb], axis=2)
    vv = jnp.concatenate([jnp.concatenate([vpad, vb[:, :-1]], axis=1), vb], axis=2)
    logits = jnp.einsum('bnqkgd,bnskd->bnkgqs', qb, kk,
                        preferred_element_type=jnp.float32) * (Dh ** -0.5)
    qi = jnp.arange(BLOCK)[:, None]
    si = jnp.arange(2 * BLOCK)[None, :]
    dist = qi + BLOCK - si
    in_window = (dist >= 0) & (dist < WINDOW)
    bias = rel_bias.astype(jnp.float32)[t5_bucket(dist)]
    bias = bias.transpose(2, 0, 1).reshape(N_KV_HEADS, G, BLOCK, 2 * BLOCK)
    valid = in_window[None] & ((jnp.arange(nb)[:, None, None] > 0) | (si[None] >= BLOCK))
    logits = jnp.where(valid[None, :, None, None], logits + bias[None, None], NEG_INF)
    sink = sinks.astype(jnp.float32).reshape(N_KV_HEADS, G)[None, None, :, :, None, None]
    m = jnp.maximum(jnp.max(logits, axis=-1, keepdims=True), sink)
    p = jnp.exp(logits - m)
    p = p / (jnp.sum(p, axis=-1, keepdims=True) + jnp.exp(sink - m))
    out = jnp.einsum('bnkgqs,bnskd->bnqkgd', p.astype(v.dtype), vv)
    return out.reshape(B, S, H * Dh)


def chunked_spatial_gating(u, v, ln_g, ln_b, w_s, b_s):
    B, S, _ = u.shape
    nc = S // CHUNK
    G, Dg = N_GMLP_GROUPS, GMLP_GROUP_DIM
    u = jax.nn.gelu(u).reshape(B, nc, CHUNK, G, Dg)
    v = layer_norm(jax.nn.gelu(v).reshape(B, S, G, Dg), ln_g.reshape(G, Dg), ln_b.reshape(G, Dg))
    v = v.reshape(B, nc, CHUNK, G, Dg)
    causal = jnp.tril(jnp.ones((CHUNK, CHUNK), dtype=bool))
    w = jnp.where(causal[None], w_s, jnp.zeros_like(w_s))
    mixed = jnp.einsum('gts,bnsgc->bntgc', w, v) + b_s.T[None, None, :, :, None]
    return (u * mixed).reshape(B, S, GMLP_WIDTH)


def _normal(key, shape, scale):
    return jax.random.normal(key, shape, dtype=jnp.float32) * scale


def setup_inputs(seed: int = 0) -> dict:
    key = jax.random.key(seed)
    ks = jax.random.split(key, 24)
    L = DEPTH
    w_in = _normal(ks[5], (L, D_MODEL, IN_WIDTH), D_MODEL ** -0.5)
    v_lo, v_hi = ATTN_WIDTH + KV_WIDTH, ATTN_WIDTH + 2 * KV_WIDTH
    w_in = w_in.at[:, :, v_lo:v_hi].multiply(BETA)
    return {
        "x": _normal(ks[0], (BATCH, SEQ, D_MODEL), 1.0),
        "c": _normal(ks[1], (BATCH, D_MODEL), 1.0),
        "rel_bias": _normal(ks[2], (N_BUCKETS, N_ATTN_HEADS), 0.5),
        "w_ada": _normal(ks[3], (L, D_MODEL, 6 * D_MODEL), 0.5 * D_MODEL ** -0.5),
        "b_ada": _normal(ks[4], (L, 6 * D_MODEL), 0.01),
        "w_in": w_in,
        "b_in": _normal(ks[6], (L, IN_WIDTH), 0.01),
        "attn_sinks": _normal(ks[7], (L, N_ATTN_HEADS), 0.5),
        "gmlp_ln_g": 1.0 + _normal(ks[8], (L, GMLP_WIDTH), 0.01),
        "gmlp_ln_b": _normal(ks[9], (L, GMLP_WIDTH), 0.01),
        "gmlp_w_s": _normal(ks[10], (L, N_GMLP_GROUPS, CHUNK, CHUNK), CHUNK ** -0.5),
        "gmlp_b_s": 1.0 + _normal(ks[11], (L, N_GMLP_GROUPS, CHUNK), 0.01),
        "attn_out_g": 1.0 + _normal(ks[12], (L, ATTN_WIDTH), 0.01),
        "gmlp_out_g": 1.0 + _normal(ks[13], (L, GMLP_WIDTH), 0.01),
        "w_out": _normal(ks[14], (L, MIX_WIDTH, D_MODEL), BETA * MIX_WIDTH ** -0.5),
        "ln1_g": 1.0 + _normal(ks[15], (L, D_MODEL), 0.01),
        "ln1_b": _normal(ks[16], (L, D_MODEL), 0.01),
        "w_gate_up": _normal(ks[17], (L, D_MODEL, 2 * D_FF), D_MODEL ** -0.5),
        "w_down": _normal(ks[18], (L, D_FF, D_MODEL), BETA * D_FF ** -0.5),
        "ln2_g": 1.0 + _normal(ks[19], (L, D_MODEL), 0.01),
        "ln2_b": _normal(ks[20], (L, D_MODEL), 0.01),
    }


def reference(x, c, rel_bias, w_ada, b_ada, w_in, b_in, attn_sinks, gmlp_ln_g, gmlp_ln_b,
              gmlp_w_s, gmlp_b_s, attn_out_g, gmlp_out_g, w_out, ln1_g, ln1_b,
              w_gate_up, w_down, ln2_g, ln2_b):
    B, S, _ = x.shape
    splits = [ATTN_WIDTH, ATTN_WIDTH + KV_WIDTH, ATTN_WIDTH + 2 * KV_WIDTH,
              ATTN_WIDTH + 2 * KV_WIDTH + GMLP_WIDTH]
    for layer in range(DEPTH):
        mod = jax.nn.silu(c) @ w_ada[layer] + b_ada[layer]
        sh1, sc1, g1, sh2, sc2, g2 = jnp.split(mod[:, None, :], 6, axis=-1)

        h = x * (1.0 + sc1) + sh1
        proj = h @ w_in[layer] + b_in[layer]
        q, k, v, gu, gv = jnp.split(proj, splits, axis=-1)
        attn = sliding_window_attention(
            q.reshape(B, S, N_ATTN_HEADS, HEAD_DIM),
            k.reshape(B, S, N_KV_HEADS, HEAD_DIM),
            v.reshape(B, S, N_KV_HEADS, HEAD_DIM),
            attn_sinks[layer], rel_bias)
        gm = chunked_spatial_gating(gu, gv, gmlp_ln_g[layer], gmlp_ln_b[layer],
                                    gmlp_w_s[layer], gmlp_b_s[layer])
        mixed = jnp.concatenate([rms_norm(attn, attn_out_g[layer]),
                                 rms_norm(gm, gmlp_out_g[layer])], axis=-1)
        y = mixed @ w_out[layer]
        x = layer_norm(ALPHA * x + g1 * y, ln1_g[layer], ln1_b[layer])

        h = x * (1.0 + sc2) + sh2
        gate, up = jnp.split(h @ w_gate_up[layer], 2, axis=-1)
        y = (jax.nn.silu(gate) * up) @ w_down[layer]
        x = layer_norm(ALPHA * x + g2 * y, ln2_g[layer], ln2_b[layer])
    return x
```

```python
import contextlib
import numpy as np
import ml_dtypes
import concourse.bass as bass
import concourse.mybir as mybir
from concourse.bass_utils import run_bass_kernel_spmd

F32 = mybir.dt.float32
BF16 = mybir.dt.bfloat16
AF = mybir.ActivationFunctionType
ALU = mybir.AluOpType
AX = mybir.AxisListType

D = 1024
NH = 8
DFF = 2816
NPAIR = DFF // 128
ALPHA = 2 ** 0.25
EPS = 1e-5
NSLOT = 3
import os
STORE_ENG = os.environ.get("K_STORE_ENG", "sp")
TB = 4
T = TB * 128
KW = 128 + T


class Tok:
    __slots__ = ("key", "sem", "val")

    def __init__(self, key, sem, val):
        self.key, self.sem, self.val = key, sem, val


class Buf:
    def __init__(self, name):
        self.name = name
        self.last_w = None
        self.readers = {}
        self.pending = None
        self.ld = None
        self.st = None


class Eng:
    def __init__(self, name, sem):
        self.name, self.sem = name, sem
        self.cnt = 0
        self.seen = {}
        self.prog = []
        self.pend_r = []
        self.pend_w = []


class Sched:
    def __init__(self, nc, es):
        self.nc = nc
        self.es = es
        self.engs = {}
        for n in ("pe", "act", "dve", "pool", "sp"):
            self.engs[n] = Eng(n, es.enter_context(nc.semaphore("sem_" + n)))
        self.ndma = 0

    def _wait(self, eng, tok):
        if tok is None:
            return
        if eng.name == "pe" and tok.key == "pe":
            return
        if eng.seen.get(tok.key, 0) >= tok.val:
            return
        eng.seen[tok.key] = tok.val
        eng.prog.append(("wait", tok.sem, tok.val))

    def _deps(self, eng, reads, writes):
        for b in list(reads) + list(writes):
            assert b.pending is None or b.pending is eng, (b.name, eng.name)
        for b in reads:
            self._wait(eng, b.last_w)
        for b in writes:
            self._wait(eng, b.last_w)
            for t in list(b.readers.values()):
                self._wait(eng, t)

    def op(self, en, fn, reads=(), writes=(), inc=True):
        eng = self.engs[en]
        self._deps(eng, reads, writes)
        eng.pend_r.extend(reads)
        eng.pend_w.extend(writes)
        for b in list(reads) + list(writes):
            b.pending = eng
        if inc:
            eng.cnt += 1
            tok = Tok(eng.name, eng.sem, eng.cnt)
            eng.prog.append(("op", fn, eng.sem, 1))
            for b in eng.pend_r:
                b.readers[eng.name] = tok
                b.pending = None
            for b in eng.pend_w:
                b.last_w = tok
                b.readers = {}
                b.pending = None
            eng.pend_r, eng.pend_w = [], []
            return tok
        eng.prog.append(("op", fn, None, 0))
        return None

    def _dsem(self, buf, kind):
        d = getattr(buf, kind)
        if d is None:
            self.ndma += 1
            d = [self.es.enter_context(self.nc.semaphore("d%s_%d" % (kind, self.ndma))), 0,
                 "%s:%s:%d" % (kind, buf.name, self.ndma)]
            setattr(buf, kind, d)
        return d

    def dma(self, fn, reads=(), writes=(), owner=None, kind="ld", en="sp"):
        eng = self.engs[en]
        self._deps(eng, reads, writes)
        d = self._dsem(owner, kind)
        d[1] += 16
        tok = Tok(d[2], d[0], d[1])
        eng.prog.append(("op", fn, d[0], 16))
        for b in reads:
            b.readers[d[2]] = tok
        for b in writes:
            b.last_w = tok
            b.readers = {}
        return tok

    def barrier(self, bufs):
        for e in self.engs.values():
            assert not e.pend_r and not e.pend_w
        toks = [Tok(o.name, o.sem, o.cnt) for o in self.engs.values() if o.cnt > 0]
        for b in bufs:
            for d in (b.ld, b.st):
                if d is not None:
                    toks.append(Tok(d[2], d[0], d[1]))
        for e in self.engs.values():
            for t in toks:
                self._wait(e, t)

    def wait_tok(self, en, tok):
        self._wait(self.engs[en], tok)

    def replay(self, en, e):
        for it in self.engs[en].prog:
            if it[0] == "wait":
                e.wait_ge(it[1], it[2])
            else:
                ins = it[1](e)
                if it[2] is not None:
                    ins.then_inc(it[2], it[3])


def build_program(S):
    assert S % T == 0
    NT = S // T
    nc = bass.Bass("TRN2", target_bir_lowering=False)

    def din(name, shape, dt=F32):
        return nc.dram_tensor(name, list(shape), dt, kind="ExternalInput").ap()

    xT_d = din("xT", [D, S])
    x_d = din("x", [S, D])
    ccol_d = din("ccol", [128, 8])
    wada_d = din("w_ada", [D, 6 * D])
    badaT_d = din("badaT", [128, 48])
    bada_d = din("b_ada", [1, 6 * D])
    win_d = din("w_in", [D, 1792])
    bin_d = din("b_in", [1, 1792])
    bqk_d = din("bqk", [128, 5])
    sinks_d = din("sinks", [1, 8])
    biasm_d = din("biasm", [128, 8 * 256])
    mask_d = din("maskm", [128, 256])
    lng_d = din("gmlp_ln_g", [1, 512])
    lnb_d = din("gmlp_ln_b", [1, 512])
    wsT_d = din("wsT", [128, 8 * 128])
    wsN_d = din("wsN", [128, 8 * 128])
    trilT_d = din("trilT", [128, 128])
    tril_d = din("tril", [128, 128])
    bsT_d = din("bsT", [128, 8])
    gcol_d = din("gcol", [128, 8])
    wout_d = din("w_out", [D, D])
    ln1g_d = din("ln1_g", [1, D])
    ln1b_d = din("ln1_b", [1, D])
    wgu_d = din("w_gate_up", [D, 2 * DFF])
    wdn_d = din("w_down", [DFF, D])
    ln2g_d = din("ln2_g", [1, D])
    ln2b_d = din("ln2_b", [1, D])
    identb_d = din("identb", [128, 128], BF16)
    identf_d = din("identf", [128, 128])
    out_d = nc.dram_tensor("out", [S, D], F32, kind="ExternalOutput").ap()
    wgu_s = nc.dram_tensor("wgu_bf", [NPAIR, 128, 8 * 256], BF16, kind="Internal").ap()
    wdn_s = nc.dram_tensor("wdn_bf", [NPAIR, 128, D], BF16, kind="Internal").ap()

    es = contextlib.ExitStack()
    with es:
        sc = Sched(nc, es)

        def sb(name, shape, dt=F32):
            return es.enter_context(nc.sbuf_tensor("s_" + name, list(shape), dt))

        def psb(name, shape, dt=F32):
            return es.enter_context(nc.psum_tensor("p_" + name, list(shape), dt))

        arenaA = sb("arenaA", [128, 4096])
        arenaB = sb("arenaB", [128, 4096])
        hhT = sb("hhT", [128, NPAIR * T], BF16)
        actT = sb("actT", [128, 8 * T], BF16)
        win_bf = sb("win_bf", [128, 8 * 1792], BF16)
        wout_bf = sb("wout_bf", [128, 8 * D], BF16)
        qT = sb("qT", [128, 4 * T], BF16)
        kT = sb("kT", [128, 2 * (128 + T)], BF16)
        vsb = [sb("vsb%d" % i, [128, 2 * 65], BF16) for i in range(3)]
        mixT = sb("mixT", [128, 8 * T], BF16)
        mixb = sb("mixb", [128, D], BF16)
        p0 = sb("p0", [128, 8 * 256], BF16)
        pt = sb("pt", [128, 8 * 256], BF16)
        e_bf = sb("e_bf", [128, 8 * 256], BF16)
        gu = sb("gu", [128, 512])
        gvv = sb("gvv", [128, 512])
        sq = sb("sq", [128, 512])
        nbf = sb("nbf", [128, 512], BF16)
        t1 = sb("t1", [128, 512])
        gam = sb("gam", [128, 512])
        c2 = sb("c2", [128, 512])
        wsT_bf = sb("wsT_bf", [128, 8 * 128], BF16)
        xtok = [sb("xtok%d" % i, [128, D]) for i in range(2)]
        ostage = [sb("ostage%d" % i, [128, D]) for i in range(2)]
        sg = [sb("sg%d" % i, [128, 512]) for i in range(2)]
        slots = [sb("slot%d" % i, [128, 2048], BF16) for i in range(NSLOT)]
        lnt = {n: sb("lnt_" + n, [128, D]) for n in ("g1", "b1", "g2", "b2")}
        identb = sb("identb", [128, 128], BF16)
        identf = sb("identf", [128, 128])
        ones_row = sb("ones_row", [1, 128], BF16)
        brow = sb("brow", [1, 1152], BF16)
        ccol = sb("ccol", [128, 8])
        scol2 = sb("scol2", [128, 16])
        screp = xtok[0]
        badaT = sb("badaT", [128, 48])
        modc = sb("modc", [128, 32])
        bqk = sb("bqk", [128, 5])
        bq8 = sb("bq8", [128, 4])
        esink = sb("esink", [128, 8])
        gcol = sb("gcol", [128, 8])
        bsT = sb("bsT", [128, 8])
        rw = sb("rw", [128, 8])
        lnb_bc = gvv
        neghalf = sb("neghalf", [128, 8])
        st_s = sb("st_s", [128, 64])
        st_a = sb("st_a", [128, 32])
        st_g = sb("st_g", [128, 32])
        st_l = sb("st_l", [128, 8])
        attnf = sb("attnf", [128, 512])
        junk = sb("junk", [128, 512], BF16)
        bnst = sb("bnst", [128, 12])
        bnmv = sb("bnmv", [128, 2])

        bank = [psb("bank%d" % i, [128, 512]) for i in range(8)]

        B = {}

        def buf(name):
            if name not in B:
                B[name] = Buf(name)
            return B[name]

        bb = [buf("bank%d" % i) for i in range(8)]

        def v3(ap2d, a, b):
            return ap2d.rearrange("p (a b) -> p a b", a=a, b=b)

        def mm(out, lhsT, rhs, start, stop, reads, writes, inc, tp=None):
            def fn(e, out=out, lhsT=lhsT, rhs=rhs, start=start, stop=stop, tp=tp):
                if tp is None:
                    return e.matmul(out, lhsT=lhsT, rhs=rhs, start=start, stop=stop)
                return e.matmul(out, lhsT=lhsT, rhs=rhs, start=start, stop=stop, tile_position=tp)
            return sc.op("pe", fn, reads, writes, inc)

        def tr(out, in_, ident, reads, writes, inc):
            def fn(e, out=out, in_=in_, ident=ident):
                return e.transpose(out, in_, ident)
            return sc.op("pe", fn, reads, writes, inc)

        def act(out, in_, func, reads, writes, bias=None, scale=None, accum=None):
            def fn(e, out=out, in_=in_, func=func, bias=bias, scale=scale, accum=accum):
                kw = {}
                if bias is not None:
                    kw["bias"] = bias
                if scale is not None:
                    kw["scale"] = scale
                if accum is not None:
                    kw["accum_out"] = accum
                return e.activation(out=out, in_=in_, func=func, **kw)
            return sc.op("act", fn, reads, writes)

        def ts(en, out, in0, s1, s2, op0, op1, reads, writes):
            def fn(e, out=out, in0=in0, s1=s1, s2=s2, op0=op0, op1=op1):
                if op1 is None:
                    return e.tensor_scalar(out=out, in0=in0, scalar1=s1, scalar2=None, op0=op0)
                return e.tensor_scalar(out=out, in0=in0, scalar1=s1, scalar2=s2, op0=op0, op1=op1)
            return sc.op(en, fn, reads, writes)

        def tt(en, out, in0, in1, op, reads, writes):
            def fn(e, out=out, in0=in0, in1=in1, op=op):
                return e.tensor_tensor(out=out, in0=in0, in1=in1, op=op)
            return sc.op(en, fn, reads, writes)

        def stt(out, in0, scalar, in1, op0, op1, reads, writes):
            def fn(e, out=out, in0=in0, scalar=scalar, in1=in1, op0=op0, op1=op1):
                return e.scalar_tensor_tensor(out=out, in0=in0, scalar=scalar, in1=in1, op0=op0, op1=op1)
            return sc.op("dve", fn, reads, writes)

        def cp(en, out, in_, reads, writes):
            def fn(e, out=out, in_=in_, en=en):
                if en == "act":
                    return e.activation(out=out, in_=in_, func=AF.Identity)
                return e.tensor_copy(out=out, in_=in_)
            return sc.op(en, fn, reads, writes)

        def red(out, in_, reads, writes):
            def fn(e, out=out, in_=in_):
                return e.tensor_reduce(out=out, in_=in_, axis=AX.X, op=ALU.add)
            return sc.op("dve", fn, reads, writes)

        def memset(en, ap, val, writes):
            def fn(e, ap=ap, val=val):
                return e.memset(ap, val)
            return sc.op(en, fn, (), writes)

        def load(out, in_, writes, owner, reads=(), nonc=False):
            def fn(e, out=out, in_=in_, nonc=nonc):
                if nonc:
                    return e.dma_start(out=out, in_=in_, allow_slow_non_contiguous=True)
                return e.dma_start(out=out, in_=in_)
            return sc.dma(fn, reads, writes, owner=owner, kind="ld")

        def store(out, in_, reads, writes, owner, en="sp"):
            def fn(e, out=out, in_=in_):
                return e.dma_start(out=out, in_=in_)
            return sc.dma(fn, reads, writes, owner=owner, kind="st", en=en)

        def rsqrt_eps(dst, src, n, reads_writes):
            ts("dve", dst, src, EPS, None, ALU.add, None, reads_writes, reads_writes)
            tt("pool", dst, dst, neghalf[:, 0:n], ALU.pow, reads_writes + [buf("neghalf")], reads_writes)

        stg = [arenaA, arenaB]
        stgb = [buf("arenaA"), buf("arenaB")]
        small = buf("small")
        bhh = buf("hhT")

        load(ccol[:], ccol_d[:, :], [small], small)
        load(badaT[:], badaT_d[:, :], [small], small)
        load(bqk[:], bqk_d[:, :], [small], small)
        load(gcol[:], gcol_d[:, :], [small], small)
        load(bsT[:], bsT_d[:, :], [small], small)
        load(identb[:], identb_d[:, :], [small], small)
        load(identf[:], identf_d[:, :], [small], small)
        load(esink[:], sinks_d.partition_broadcast(128), [small], small)
        load(lnb_bc[:], lnb_d.partition_broadcast(128), [small], small)
        load(gam[:], lng_d.partition_broadcast(128), [small], small)
        browf = arenaA[0:1, 2304:3456]
        load(browf, bin_d[0:1, 640:1792], [stgb[0]], stgb[0])
        load(lnt["g1"][:], ln1g_d.partition_broadcast(128), [small], small)
        load(lnt["b1"][:], ln1b_d.partition_broadcast(128), [small], small)
        load(lnt["g2"][:], ln2g_d.partition_broadcast(128), [small], small)
        load(lnt["b2"][:], ln2b_d.partition_broadcast(128), [small], small)

        memset("dve", neghalf[:], -0.5, [buf("neghalf")])
        memset("dve", ones_row[:], 1.0, [small])
        for i in range(3):
            memset("dve", vsb[i][:], 1.0, [buf("vsb%d" % i)])
        memset("dve", p0[:], 0.0, [buf("p0")])
        memset("dve", kT[:], 0.0, [buf("kT")])
        cp("dve", brow[:], browf, [stgb[0]], [small])
        act(ccol[:], ccol[:], AF.Silu, [small], [small])
        act(esink[:], esink[:], AF.Exp, [small], [small])
        ts("dve", bq8[:], bqk[:, 0:4], 0.125, None, ALU.mult, None, [small], [small])
        for kt in range(8):
            cp("dve", scol2[:, 2 * kt:2 * kt + 2], ccol[:, kt:kt + 1].to_broadcast([128, 2]), [small], [small])
            cp("dve", screp[:, kt * 128:(kt + 1) * 128], ccol[:, kt:kt + 1].to_broadcast([128, 128]), [small], [small])

        load(arenaA[:, 0:2048], biasm_d[:, :], [stgb[0]], stgb[0])
        load(arenaA[:, 2048:2304], mask_d[:, :], [stgb[0]], stgb[0])
        act(arenaA[:, 0:2048], arenaA[:, 0:2048], AF.Exp, [stgb[0]], [stgb[0]])
        tt("dve", v3(e_bf[:], 8, 256), v3(arenaA[:, 0:2048], 8, 256),
           arenaA[:, 2048:2304].unsqueeze(1).to_broadcast([128, 8, 256]), ALU.mult, [stgb[0]], [buf("e_bf")])

        load(arenaB[:, 0:1024], wsT_d[:, :], [stgb[1]], stgb[1])
        load(arenaB[:, 1024:1152], trilT_d[:, :], [stgb[1]], stgb[1])
        load(arenaB[:, 2048:3072], wsN_d[:, :], [stgb[1]], stgb[1])
        load(arenaB[:, 3072:3200], tril_d[:, :], [stgb[1]], stgb[1])
        tt("dve", v3(wsT_bf[:], 8, 128), v3(arenaB[:, 0:1024], 8, 128),
           arenaB[:, 1024:1152].unsqueeze(1).to_broadcast([128, 8, 128]), ALU.mult, [stgb[1]], [buf("wsT_bf")])
        tt("dve", v3(arenaB[:, 2048:3072], 8, 128), v3(arenaB[:, 2048:3072], 8, 128),
           arenaB[:, 3072:3200].unsqueeze(1).to_broadcast([128, 8, 128]), ALU.mult, [stgb[1]], [stgb[1]])
        red(rw[:], v3(arenaB[:, 2048:3072], 8, 128), [stgb[1]], [small])
        for g in range(8):
            ts("dve", c2[:, g * 64:(g + 1) * 64], lnb_bc[:, g * 64:(g + 1) * 64], rw[:, g:g + 1], bsT[:, g:g + 1],
               ALU.mult, ALU.add, [small], [buf("c2")])

        g1bc = ostage[1]
        g2bc = ostage[0]
        load(g1bc[:], bada_d[0:1, 2 * D:3 * D].partition_broadcast(128), [buf("ostage1")], buf("ostage1"))
        load(g2bc[:], bada_d[0:1, 5 * D:6 * D].partition_broadcast(128), [buf("ostage0")], buf("ostage0"))
        colmap = {0: 0, 1: 0, 2: 8, 3: 8, 6: 16, 7: 16, 8: 24, 9: 24}
        for ch in range(12):
            sl = ch % 2
            sv = v3(stg[sl][:], 8, 512)
            load(sv, wada_d[:, ch * 512:(ch + 1) * 512].rearrange("(k p) n -> p k n", p=128), [stgb[sl]], stgb[sl])
            if ch in colmap:
                for ft in range(4):
                    for kt in range(8):
                        mm(bank[0][:, 2 * ft:2 * ft + 2], sv[:, kt, ft * 128:(ft + 1) * 128], scol2[:, 2 * kt:2 * kt + 2],
                           kt == 0, kt == 7, [stgb[sl], small], [bb[0]], kt == 7 and ft == 3)
                c0 = colmap[ch] + (ch % 2) * 4
                j0 = ch * 4
                tt("dve", modc[:, c0:c0 + 4], v3(bank[0][:, 0:8], 4, 2)[:, :, 0], badaT[:, j0:j0 + 4], ALU.add,
                   [bb[0], small], [small])
            else:
                half = ch % 2
                dst = g1bc if ch < 6 else g2bc
                dbuf = buf("ostage1") if ch < 6 else buf("ostage0")
                for kt in range(8):
                    mm(bank[1][:, :], screp[:, kt * 128:(kt + 1) * 128], sv[:, kt, :], kt == 0, kt == 7,
                       [stgb[sl], small], [bb[1]], kt == 7)
                tt("dve", dst[:, half * 512:(half + 1) * 512], dst[:, half * 512:(half + 1) * 512], bank[1][:, :], ALU.add,
                   [bb[1], dbuf], [dbuf])
        ts("dve", modc[:, 8:16], modc[:, 8:16], 1.0, None, ALU.add, None, [small], [small])
        ts("dve", modc[:, 24:32], modc[:, 24:32], 1.0, None, ALU.add, None, [small], [small])

        for kt in range(8):
            sl = kt % 2
            load(stg[sl][:, 0:1792], win_d[kt * 128:(kt + 1) * 128, :], [stgb[sl]], stgb[sl])
            dstq = win_bf[:, kt * 1792:kt * 1792 + 512].rearrange("p (j g d) -> p j g d", j=4, g=2, d=64)
            srcq = stg[sl][:, 0:512].rearrange("p (g j d) -> p j g d", g=2, j=4, d=64)
            cp("dve", dstq, srcq, [stgb[sl]], [buf("win_bf")])
            cp("pool", win_bf[:, kt * 1792 + 512:(kt + 1) * 1792], stg[sl][:, 512:1792], [stgb[sl]], [buf("win_bf")])
        for kt in range(8):
            sl = kt % 2
            load(stg[sl][:, 0:1024], wout_d[kt * 128:(kt + 1) * 128, :], [stgb[sl]], stgb[sl])
            stt(wout_bf[:, kt * D:(kt + 1) * D], stg[sl][:, 0:1024], gcol[:, kt:kt + 1], g1bc[:], ALU.mult, ALU.mult,
                [stgb[sl], small, buf("ostage1")], [buf("wout_bf")])
        wgu_v = wgu_s.rearrange("i p (k c) -> i p k c", k=8, c=256)
        bwgu = buf("wgu_s")
        cnt = 0
        for kt in range(8):
            for half in range(2):
                sl = cnt % 2
                hs = hhT[:, sl * 2816:(sl + 1) * 2816]
                load(stg[sl][:, 0:2816], wgu_d[kt * 128:(kt + 1) * 128, half * DFF:(half + 1) * DFF], [stgb[sl]], stgb[sl])
                hb = buf("hh_s%d" % sl)
                cp("act" if cnt % 2 else "dve", hs, stg[sl][:, 0:2816], [stgb[sl]], [hb])
                for q4 in range(2):
                    i0 = q4 * 11
                    store(wgu_v[i0:i0 + 11, :, kt, half * 128:(half + 1) * 128].rearrange("i p c -> p i c"),
                          hs[:, i0 * 128:(i0 + 11) * 128].rearrange("p (i c) -> p i c", i=11, c=128), [hb], [bwgu], hb)
                cnt += 1
        bwdn = buf("wdn_s")
        for kt in range(NPAIR):
            sl = kt % 2
            hs = hhT[:, 6144 + sl * 1024:6144 + (sl + 1) * 1024]
            hb = buf("hd_s%d" % sl)
            load(stg[sl][:, 0:1024], wdn_d[kt * 128:(kt + 1) * 128, :], [stgb[sl]], stgb[sl])
            tt("dve" if kt % 2 else "pool", hs, stg[sl][:, 0:1024], g2bc[:], ALU.mult, [stgb[sl], buf("ostage0")], [hb])
            store(wdn_s[kt], hs, [hb], [bwdn], hb)

        sc.barrier(list(B.values()))
        bxT, bx1 = stgb[0], stgb[1]
        bact, bqT, bkT, bmixT, bmixb = buf("actT"), buf("qT"), buf("kT"), buf("mixT"), buf("mixb")
        bp0, bpt, battnf = buf("p0"), buf("pt"), buf("attnf")
        bgu, bgvv, bsq, bnbf, bt1 = buf("gu"), buf("gvv"), buf("sq"), buf("nbf"), buf("t1")
        bjunk = buf("junk")
        bsta, bstg, bstl = buf("st_a"), buf("st_g"), buf("st_l")
        bxtok = [buf("xtok0"), buf("xtok1")]
        bost = [buf("ostage0"), buf("ostage1")]
        bsg = [buf("sg0"), buf("sg1")]
        bslot = [buf("slot%d" % i) for i in range(NSLOT)]
        bvs = [buf("vsb%d" % i) for i in range(3)]
        bwin, bwout, be = buf("win_bf"), buf("wout_bf"), buf("e_bf")
        bc2, bws = buf("c2"), buf("wsT_bf")
        hT3 = v3(actT[:], 8, T)
        qT3 = v3(qT[:], 4, T)
        mixT3 = v3(mixT[:], 8, T)
        hh3 = v3(hhT[:], NPAIR, T)
        win3 = v3(win_bf[:], 8, 1792)
        wout3 = v3(wout_bf[:], 8, D)
        x13 = v3(arenaB[:], TB, D)
        xT3 = v3(arenaA[:], 8, T)
        p03 = v3(p0[:], 8, 256)
        pt3 = v3(pt[:], 8, 256)
        slot_i = [0]
        out_toks = []
        pend_st = []

        def flush_stores(force=False):
            keep = []
            for it in pend_st:
                it[0] -= 1
                if force or it[0] <= 0:
                    out_toks.append(it[1]())
                else:
                    keep.append(it)
            pend_st[:] = keep

        def next_slot():
            i = slot_i[0] % NSLOT
            slot_i[0] += 1
            flush_stores()
            return i

        def bc_last(ap2, n, m):
            return ap2.unsqueeze(2).to_broadcast([128, n, m])

        def pool_rsqrt(dst, src, n, b_):
            ts("dve", dst, src, EPS, None, ALU.add, None, [b_], [b_])
            tt("pool", dst, dst, neghalf[:, 0:n], ALU.pow, [b_, buf("neghalf")], [b_])

        def layer_norm_block(src, srcbuf, gt, bt, dst, dstbuf):
            for h in range(2):
                def fn(e, h=h):
                    return e.bn_stats(out=bnst[:, 6 * h:6 * h + 6], in_=src[:, h * 512:(h + 1) * 512])
                sc.op("dve", fn, [srcbuf], [bstl])

            def fn2(e):
                return e.bn_aggr(out=bnmv[:], in_=bnst[:])
            sc.op("dve", fn2, [bstl], [bstl])
            pool_rsqrt(st_l[:, 0:1], bnmv[:, 1:2], 1, bstl)
            ts("dve", st_l[:, 1:2], bnmv[:, 0:1], -1.0, None, ALU.mult, None, [bstl], [bstl])
            stt(src, src, st_l[:, 1:2], gt[:], ALU.add, ALU.mult, [srcbuf, bstl, small], [srcbuf])
            stt(dst, src, st_l[:, 0:1], bt[:], ALU.mult, ALU.add, [srcbuf, bstl, small], [dstbuf])

        def stage_a1(ti, b):
            t0 = ti * T
            gb = ti * TB + b
            tk = slice(b * 128, (b + 1) * 128)
            vcur = gb % 3
            xs = gb % 2
            load(xtok[xs][:], x_d[t0 + b * 128:t0 + (b + 1) * 128, :], [bxtok[xs]], bxtok[xs])
            for (c0, ncol, bk_, boff) in ((640, 128, 5, 0), (1280, 512, 1, 640), (768, 512, 0, 128)):
                o = bank[bk_][:, 256:384] if ncol == 128 else bank[bk_][:, :]
                for kt in range(8):
                    mm(o, hT3[:, kt, tk], win3[:, kt, c0:c0 + ncol], kt == 0, False, [bwin, bact], [bb[bk_]], False)
                mm(o, ones_row[0:1, :], brow[0:1, boff:boff + ncol], False, True, [small], [bb[bk_]], True)
            cp("dve", v3(vsb[vcur][:], 2, 65)[:, :, 0:64], v3(bank[5][:, 256:384], 2, 64), [bb[5]], [bvs[vcur]])
            act(gvv[:], bank[1][:, :], AF.Gelu_apprx_tanh, [bb[1]], [bgvv])
            act(gu[:], bank[0][:, :], AF.Gelu_apprx_tanh, [bb[0]], [bgu])
            hv = ([0] if gb > 0 else []) + [1]
            for j in range(4):
                bx_, by_ = 2 + 2 * (j % 2), 3 + 2 * (j % 2)
                for hf in hv:
                    kc = slice(b * 128 + hf * 128, b * 128 + hf * 128 + 128)
                    last = hf == 1
                    kc2 = slice(KW + b * 128 + hf * 128, KW + b * 128 + hf * 128 + 128)
                    mm(bank[bx_][:, hf * 128:(hf + 1) * 128], kT[:, kc], qT3[:, j, tk], True, True,
                       [bkT, bqT], [bb[bx_]], last)
                    mm(bank[by_][:, hf * 128:(hf + 1) * 128], kT[:, kc2], qT3[:, j, tk], True, True,
                       [bkT, bqT], [bb[by_]], last)
                lo = hv[0] * 128
                act(p03[:, j, lo:256], bank[bx_][:, lo:256], AF.Exp, [bb[bx_]], [bp0])
                act(p03[:, 4 + j, lo:256], bank[by_][:, lo:256], AF.Exp, [bb[by_]], [bp0])

        def stage_a2(ti, b):
            gb = ti * TB + b
            tk = slice(b * 128, (b + 1) * 128)
            vcur, vprev = gb % 3, (gb - 1) % 3
            gvv3 = v3(gvv[:], 8, 64)
            red(st_g[:, 0:8], gvv3, [bgvv], [bstg])
            tt("pool", sq[:], gvv[:], gvv[:], ALU.mult, [bgvv], [bsq])
            red(st_g[:, 8:16], v3(sq[:], 8, 64), [bsq], [bstg])
            ts("dve", st_g[:, 0:8], st_g[:, 0:8], 1.0 / 64, None, ALU.mult, None, [bstg], [bstg])
            tt("dve", st_g[:, 16:24], st_g[:, 0:8], st_g[:, 0:8], ALU.mult, [bstg], [bstg])
            stt(st_g[:, 8:16], st_g[:, 8:16], 1.0 / 64, st_g[:, 16:24], ALU.mult, ALU.subtract, [bstg], [bstg])
            pool_rsqrt(st_g[:, 8:16], st_g[:, 8:16], 8, bstg)
            tt("dve", pt[:], p0[:], e_bf[:], ALU.mult, [bp0, be], [bpt])
            for h in range(8):
                g = h // 4
                o = bank[6 + g][:, (h % 4) * 65:(h % 4) * 65 + 65]
                if gb > 0:
                    mm(o, pt3[:, h, 0:128], v3(vsb[vprev][:], 2, 65)[:, g, :], True, False, [bpt, bvs[vprev]], [bb[6 + g]], False)
                mm(o, pt3[:, h, 128:256], v3(vsb[vcur][:], 2, 65)[:, g, :], gb == 0, True, [bpt, bvs[vcur]], [bb[6 + g]],
                   h % 4 == 3)
            tt("dve", gvv3, gvv3, bc_last(st_g[:, 0:8], 8, 64), ALU.subtract, [bgvv, bstg], [bgvv])
            tt("dve", v3(nbf[:], 8, 64), gvv3, bc_last(st_g[:, 8:16], 8, 64), ALU.mult, [bgvv, bstg], [bnbf])
            for g in range(8):
                mm(bank[2][:, g * 64:(g + 1) * 64], v3(wsT_bf[:], 8, 128)[:, g, :], nbf[:, g * 64:(g + 1) * 64], True, True,
                   [bws, bnbf], [bb[2]], g == 7)
            for g in range(2):
                tt("dve", st_a[:, 4 * g:4 * g + 4], v3(bank[6 + g][:, 0:260], 4, 65)[:, :, 64], esink[:, 4 * g:4 * g + 4],
                   ALU.add, [bb[6 + g], small], [bsta])

            def fnr(e):
                return e.reciprocal(out=st_a[:, 8:16], in_=st_a[:, 0:8])
            sc.op("dve", fnr, [bsta], [bsta])
            for g in range(2):
                tt("dve", v3(attnf[:, g * 256:(g + 1) * 256], 4, 64), v3(bank[6 + g][:, 0:260], 4, 65)[:, :, 0:64],
                   bc_last(st_a[:, 8 + 4 * g:12 + 4 * g], 4, 64), ALU.mult, [bb[6 + g], bsta], [battnf])
            act(junk[:], attnf[:], AF.Square, [battnf], [bjunk, bsta], accum=st_a[:, 16:17])
            ts("dve", st_a[:, 17:18], st_a[:, 16:17], 1.0 / 512, EPS, ALU.mult, ALU.add, [bsta], [bsta])
            tt("pool", st_a[:, 17:18], st_a[:, 17:18], neghalf[:, 0:1], ALU.pow, [bsta, buf("neghalf")], [bsta])
            act(mixb[:, 0:512], attnf[:], AF.Identity, [battnf, bsta], [bmixb], scale=st_a[:, 17:18])
            tt("dve", t1[:], bank[2][:, :], gam[:], ALU.mult, [bb[2], small], [bt1])
            tt("dve", t1[:], t1[:], c2[:], ALU.add, [bt1, bc2], [bt1])
            tt("dve", t1[:], t1[:], gu[:], ALU.mult, [bt1, bgu], [bt1])
            act(junk[:], t1[:], AF.Square, [bt1], [bjunk, bstg], accum=st_g[:, 24:25])
            ts("dve", st_g[:, 25:26], st_g[:, 24:25], 1.0 / 512, EPS, ALU.mult, ALU.add, [bstg], [bstg])
            tt("pool", st_g[:, 25:26], st_g[:, 25:26], neghalf[:, 0:1], ALU.pow, [bstg, buf("neghalf")], [bstg])
            act(mixb[:, 512:1024], t1[:], AF.Identity, [bt1, bstg], [bmixb], scale=st_g[:, 25:26])
            trp = bank[3][:, :].bitcast(BF16)
            for kt in range(8):
                tr(trp[:, kt * 128:(kt + 1) * 128], mixb[:, kt * 128:(kt + 1) * 128], identb[:], [bmixb, small], [bb[3]], kt == 7)
            cp("act", mixT3[:, :, tk], v3(trp, 8, 128), [bb[3]], [bmixT])
            for half in range(2):
                for kt in range(8):
                    mm(bank[6 + half][:, :], mixT3[:, kt, tk], wout3[:, kt, half * 512:(half + 1) * 512], kt == 0, kt == 7,
                       [bmixT, bwout], [bb[6 + half]], kt == 7)

        def stage_b(ti, b):
            gb = ti * TB + b
            tk = slice(b * 128, (b + 1) * 128)
            xs = gb % 2
            for half in range(2):
                hs_ = slice(half * 512, (half + 1) * 512)
                stt(xtok[xs][:, hs_], xtok[xs][:, hs_], ALPHA, bank[6 + half][:, :], ALU.mult, ALU.add,
                    [bxtok[xs], bb[6 + half]], [bxtok[xs]])
            layer_norm_block(xtok[xs][:], bxtok[xs], lnt["g1"], lnt["b1"], x13[:, b, :], bx1)
            for kt in range(8):
                bk_ = 4 + kt // 4
                tr(bank[bk_][:, (kt % 4) * 128:(kt % 4 + 1) * 128], x13[:, b, kt * 128:(kt + 1) * 128], identf[:],
                   [bx1, small], [bb[bk_]], kt % 4 == 3)
            for kt in range(8):
                bk_ = 4 + kt // 4
                src = bank[bk_][:, (kt % 4) * 128:(kt % 4 + 1) * 128]
                if kt % 2:
                    ts("dve", hT3[:, kt, tk], src, modc[:, 24 + kt:25 + kt], modc[:, 16 + kt:17 + kt], ALU.mult, ALU.add,
                       [bb[bk_], small], [bact])
                else:
                    act(hT3[:, kt, tk], src, AF.Identity, [bb[bk_], small], [bact], bias=modc[:, 16 + kt:17 + kt],
                        scale=modc[:, 24 + kt:25 + kt])

        load(xT3, xT_d[:, 0:T].rearrange("(k p) t -> p k t", p=128), [bxT], bxT)
        for ti in range(NT):
            t0 = ti * T
            for kt in range(8):
                if kt % 2:
                    ts("dve", hT3[:, kt, :], xT3[:, kt, :], modc[:, 8 + kt:9 + kt], modc[:, kt:kt + 1], ALU.mult, ALU.add,
                       [bxT, small], [bact])
                else:
                    act(hT3[:, kt, :], xT3[:, kt, :], AF.Identity, [bxT, small], [bact], bias=modc[:, kt:kt + 1],
                        scale=modc[:, 8 + kt:9 + kt])
            if ti + 1 < NT:
                load(xT3, xT_d[:, t0 + T:t0 + 2 * T].rearrange("(k p) t -> p k t", p=128), [bxT], bxT)
            for m in range(5):
                bk_ = m % 2
                for kt in range(8):
                    mm(bank[bk_][:, :], win3[:, kt, m * 128:(m + 1) * 128], hT3[:, kt, :], kt == 0, kt == 7,
                       [bwin, bact], [bb[bk_]], kt == 7)
                if m < 4:
                    act(qT3[:, m, :], bank[bk_][:, :], AF.Identity, [bb[bk_], small], [bqT], bias=bq8[:, m:m + 1], scale=0.125)
                else:
                    act(kT[0:64, 128:128 + T], bank[bk_][0:64, :], AF.Identity, [bb[bk_], small], [bkT], bias=bqk[0:64, 4:5])
                    act(kT[64:128, KW + 128:KW + 128 + T], bank[bk_][64:128, :], AF.Identity, [bb[bk_], small], [bkT],
                        bias=bqk[64:128, 4:5])
            for s_ in range(TB + 1):
                if s_ < TB:
                    stage_a1(ti, s_)
                if s_ >= 1:
                    stage_b(ti, s_ - 1)
                if s_ < TB:
                    stage_a2(ti, s_)
            cp("dve", kT[:, 0:128], kT[:, T:T + 128], [bkT], [bkT])
            cp("dve", kT[:, KW:KW + 128], kT[:, KW + T:KW + T + 128], [bkT], [bkT])

            for p in range(NPAIR):
                si = next_slot()
                load(slots[si][:], wgu_s[p], [bslot[si]], bslot[si], reads=[bwgu])
                sv = v3(slots[si][:], 8, 256)
                bg, bu = (0, 1) if p % 2 == 0 else (2, 3)
                for kt in range(8):
                    mm(bank[bg][:, :], sv[:, kt, 0:128], hT3[:, kt, :], kt == 0, kt == 7, [bslot[si], bact], [bb[bg]], kt == 7)
                for kt in range(8):
                    mm(bank[bu][:, :], sv[:, kt, 128:256], hT3[:, kt, :], kt == 0, kt == 7, [bslot[si], bact], [bb[bu]], kt == 7)
                act(sg[p % 2][:], bank[bg][:, :], AF.Silu, [bb[bg]], [bsg[p % 2]])
                tt("dve", hh3[:, p, :], bank[bu][:, :], sg[p % 2][:], ALU.mult, [bb[bu], bsg[p % 2]], [bhh])
            for bp in range(2):
                for kc in range(NPAIR // 2):
                    si = next_slot()
                    load(v3(slots[si][:], 2, D), wdn_s[2 * kc:2 * kc + 2].rearrange("k p n -> p k n"), [bslot[si]], bslot[si],
                         reads=[bwdn])
                    sv = v3(slots[si][:], 2, D)
                    for kk in range(2):
                        kt = 2 * kc + kk
                        for bl in range(2):
                            b = 2 * bp + bl
                            for half in range(2):
                                bk_ = 4 + bl * 2 + half
                                mm(bank[bk_][:, :], hh3[:, kt, b * 128:(b + 1) * 128], sv[:, kk, half * 512:(half + 1) * 512],
                                   kt == 0, kt == NPAIR - 1, [bslot[si], bhh], [bb[bk_]],
                                   (kk == 1 and bl == 1 and half == 1))
                for bl in range(2):
                    b = 2 * bp + bl
                    gb = ti * TB + b
                    for half in range(2):
                        hs_ = slice(half * 512, (half + 1) * 512)
                        stt(x13[:, b, hs_], x13[:, b, hs_], ALPHA, bank[4 + bl * 2 + half][:, :], ALU.mult, ALU.add,
                            [bx1, bb[4 + bl * 2 + half]], [bx1])
                    os_ = gb % 2
                    layer_norm_block(x13[:, b, :], bx1, lnt["g2"], lnt["b2"], ostage[os_][:], bost[os_])
                    pend_st.append([5, (lambda r0=t0 + b * 128, os_=os_: store(out_d[r0:r0 + 128, :], ostage[os_][:], [bost[os_]],
                                                                           [buf("out_d")], bost[os_], en=STORE_ENG))])

        flush_stores(force=True)
        for d in (bost[0].st, bost[1].st):
            if d is not None:
                sc.wait_tok("sp", Tok(d[2], d[0], d[1]))

        with nc.allow_non_contiguous_dma("weight re-layout into bf16 scratch"), nc.Block() as block:
            @block.sync
            def _(e):
                sc.replay("sp", e)

            @block.tensor
            def _(e):
                sc.replay("pe", e)

            @block.scalar
            def _(e):
                sc.replay("act", e)

            @block.vector
            def _(e):
                sc.replay("dve", e)

            @block.gpsimd
            def _(e):
                sc.replay("pool", e)
    return nc


def _t5_bucket(dist):
    n = np.maximum(dist, 0)
    nf = np.maximum(n, 16).astype(np.float32)
    large = 16 + (np.log(nf / np.float32(16)) / np.float32(np.log(128 / 16)) * np.float32(16)).astype(np.int32)
    large = np.minimum(large, 31)
    return np.where(n < 16, n, large)


def _const_inputs():
    s = np.arange(128)[:, None]
    c = np.arange(256)[None, :]
    q = np.where(c < 128, c, c - 128)
    dist = np.where(c < 128, q + 128 - s, q - s)
    valid = (dist >= 0) & (dist < 128)
    bucket = _t5_bucket(dist)
    tril_ts = (np.arange(128)[None, :] <= np.arange(128)[:, None]).astype(np.float32)
    return bucket, valid.astype(np.float32), tril_ts


def make_in_maps(inputs, S):
    f = lambda a: np.ascontiguousarray(np.asarray(a, dtype=np.float32))
    x = f(inputs["x"])
    Bn = x.shape[0]
    bucket, valid, tril_ts = _const_inputs()
    rel_bias = f(inputs["rel_bias"])
    biasm = np.ascontiguousarray(rel_bias[bucket].transpose(0, 2, 1)).reshape(128, 8 * 256)
    b_in = f(inputs["b_in"])[0]
    bq = b_in[0:512].reshape(2, 4, 64).transpose(0, 2, 1).reshape(128, 4)
    bqk = np.ascontiguousarray(np.concatenate([bq, b_in[512:640].reshape(128, 1)], axis=1))
    w_s = f(inputs["gmlp_w_s"])[0]
    shared = {
        "w_ada": f(inputs["w_ada"])[0],
        "badaT": np.ascontiguousarray(f(inputs["b_ada"])[0].reshape(48, 128).T),
        "b_ada": f(inputs["b_ada"]).reshape(1, -1),
        "w_in": f(inputs["w_in"])[0],
        "b_in": f(inputs["b_in"]).reshape(1, -1),
        "bqk": bqk,
        "sinks": f(inputs["attn_sinks"]).reshape(1, 8),
        "biasm": biasm,
        "maskm": valid,
        "gmlp_ln_g": f(inputs["gmlp_ln_g"]).reshape(1, 512),
        "gmlp_ln_b": f(inputs["gmlp_ln_b"]).reshape(1, 512),
        "wsT": np.ascontiguousarray(w_s.transpose(2, 0, 1)).reshape(128, 1024),
        "wsN": np.ascontiguousarray(w_s.transpose(1, 0, 2)).reshape(128, 1024),
        "trilT": np.ascontiguousarray(tril_ts.T),
        "tril": tril_ts,
        "bsT": np.ascontiguousarray(f(inputs["gmlp_b_s"])[0].T),
        "gcol": np.ascontiguousarray(np.concatenate([f(inputs["attn_out_g"])[0], f(inputs["gmlp_out_g"])[0]]).reshape(8, 128).T),
        "w_out": f(inputs["w_out"])[0],
        "ln1_g": f(inputs["ln1_g"]).reshape(1, -1),
        "ln1_b": f(inputs["ln1_b"]).reshape(1, -1),
        "w_gate_up": f(inputs["w_gate_up"])[0],
        "w_down": f(inputs["w_down"])[0],
        "ln2_g": f(inputs["ln2_g"]).reshape(1, -1),
        "ln2_b": f(inputs["ln2_b"]).reshape(1, -1),
        "identb": np.eye(128, dtype=ml_dtypes.bfloat16),
        "identf": np.eye(128, dtype=np.float32),
    }
    c = f(inputs["c"])
    maps = []
    for b in range(Bn):
        m = dict(shared)
        m["x"] = np.ascontiguousarray(x[b, :S])
        m["xT"] = np.ascontiguousarray(x[b, :S].T)
        m["ccol"] = np.ascontiguousarray(c[b].reshape(8, 128).T)
        maps.append(m)
    return maps


_NC_CACHE = {}


def kernel(**inputs):
    x = np.asarray(inputs["x"])
    Bn, S, _ = x.shape
    if S not in _NC_CACHE:
        _NC_CACHE[S] = build_program(S)
    nc = _NC_CACHE[S]
    maps = make_in_maps(inputs, S)
    res = run_bass_kernel_spmd(nc, maps, core_ids=list(range(Bn)))
    return np.stack([np.asarray(r["out"], dtype=np.float32) for r in res.results], axis=0)
```

```python
import contextlib
import numpy as np
import ml_dtypes
import concourse.bass as bass
import concourse.mybir as mybir
from concourse.bass_utils import run_bass_kernel_spmd

F32 = mybir.dt.float32
BF16 = mybir.dt.bfloat16
AF = mybir.ActivationFunctionType
ALU = mybir.AluOpType
AX = mybir.AxisListType

D = 1024
NH = 8
DFF = 2816
NPAIR = DFF // 128
ALPHA = 2 ** 0.25
EPS = 1e-5
NSLOT = 3
import os
STORE_ENG = os.environ.get("K_STORE_ENG", "sp")
TB = 4
T = TB * 128
KW = 128 + T


class Tok:
    __slots__ = ("key", "sem", "val")

    def __init__(self, key, sem, val):
        self.key, self.sem, self.val = key, sem, val


class Buf:
    def __init__(self, name):
        self.name = name
        self.last_w = None
        self.readers = {}
        self.pending = None
        self.ld = None
        self.st = None


class Eng:
    def __init__(self, name, sem):
        self.name, self.sem = name, sem
        self.cnt = 0
        self.seen = {}
        self.prog = []
        self.pend_r = []
        self.pend_w = []


class Sched:
    def __init__(self, nc, es):
        self.nc = nc
        self.es = es
        self.engs = {}
        for n in ("pe", "act", "dve", "pool", "sp"):
            self.engs[n] = Eng(n, es.enter_context(nc.semaphore("sem_" + n)))
        self.ndma = 0

    def _wait(self, eng, tok):
        if tok is None:
            return
        if eng.name == "pe" and tok.key == "pe":
            return
        if eng.seen.get(tok.key, 0) >= tok.val:
            return
        eng.seen[tok.key] = tok.val
        eng.prog.append(("wait", tok.sem, tok.val))

    def _deps(self, eng, reads, writes):
        for b in list(reads) + list(writes):
            assert b.pending is None or b.pending is eng, (b.name, eng.name)
        for b in reads:
            self._wait(eng, b.last_w)
        for b in writes:
            self._wait(eng, b.last_w)
            for t in list(b.readers.values()):
                self._wait(eng, t)

    def op(self, en, fn, reads=(), writes=(), inc=True):
        eng = self.engs[en]
        self._deps(eng, reads, writes)
        eng.pend_r.extend(reads)
        eng.pend_w.extend(writes)
        for b in list(reads) + list(writes):
            b.pending = eng
        if inc:
            eng.cnt += 1
            tok = Tok(eng.name, eng.sem, eng.cnt)
            eng.prog.append(("op", fn, eng.sem, 1))
            for b in eng.pend_r:
                b.readers[eng.name] = tok
                b.pending = None
            for b in eng.pend_w:
                b.last_w = tok
                b.readers = {}
                b.pending = None
            eng.pend_r, eng.pend_w = [], []
            return tok
        eng.prog.append(("op", fn, None, 0))
        return None

    def _dsem(self, buf, kind):
        d = getattr(buf, kind)
        if d is None:
            self.ndma += 1
            d = [self.es.enter_context(self.nc.semaphore("d%s_%d" % (kind, self.ndma))), 0,
                 "%s:%s:%d" % (kind, buf.name, self.ndma)]
            setattr(buf, kind, d)
        return d

    def dma(self, fn, reads=(), writes=(), owner=None, kind="ld", en="sp"):
        eng = self.engs[en]
        self._deps(eng, reads, writes)
        d = self._dsem(owner, kind)
        d[1] += 16
        tok = Tok(d[2], d[0], d[1])
        eng.prog.append(("op", fn, d[0], 16))
        for b in reads:
            b.readers[d[2]] = tok
        for b in writes:
            b.last_w = tok
            b.readers = {}
        return tok

    def barrier(self, bufs):
        for e in self.engs.values():
            assert not e.pend_r and not e.pend_w
        toks = [Tok(o.name, o.sem, o.cnt) for o in self.engs.values() if o.cnt > 0]
        for b in bufs:
            for d in (b.ld, b.st):
                if d is not None:
                    toks.append(Tok(d[2], d[0], d[1]))
        for e in self.engs.values():
            for t in toks:
                self._wait(e, t)

    def wait_tok(self, en, tok):
        self._wait(self.engs[en], tok)

    def replay(self, en, e):
        for it in self.engs[en].prog:
            if it[0] == "wait":
                e.wait_ge(it[1], it[2])
            else:
                ins = it[1](e)
                if it[2] is not None:
                    ins.then_inc(it[2], it[3])


def build_program(S):
    assert S % T == 0
    NT = S // T
    nc = bass.Bass("TRN2", target_bir_lowering=False)

    def din(name, shape, dt=F32):
        return nc.dram_tensor(name, list(shape), dt, kind="ExternalInput").ap()

    xT_d = din("xT", [D, S])
    x_d = din("x", [S, D])
    ccol_d = din("ccol", [128, 8])
    wada_d = din("w_ada", [D, 6 * D])
    badaT_d = din("badaT", [128, 48])
    bada_d = din("b_ada", [1, 6 * D])
    win_d = din("w_in", [D, 1792])
    bin_d = din("b_in", [1, 1792])
    bqk_d = din("bqk", [128, 5])
    sinks_d = din("sinks", [1, 8])
    biasm_d = din("biasm", [128, 8 * 256])
    mask_d = din("maskm", [128, 256])
    lng_d = din("gmlp_ln_g", [1, 512])
    lnb_d = din("gmlp_ln_b", [1, 512])
    wsT_d = din("wsT", [128, 8 * 128])
    wsN_d = din("wsN", [128, 8 * 128])
    trilT_d = din("trilT", [128, 128])
    tril_d = din("tril", [128, 128])
    bsT_d = din("bsT", [128, 8])
    gcol_d = din("gcol", [128, 8])
    wout_d = din("w_out", [D, D])
    ln1g_d = din("ln1_g", [1, D])
    ln1b_d = din("ln1_b", [1, D])
    wgu_d = din("w_gate_up", [D, 2 * DFF])
    wdn_d = din("w_down", [DFF, D])
    ln2g_d = din("ln2_g", [1, D])
    ln2b_d = din("ln2_b", [1, D])
    identb_d = din("identb", [128, 128], BF16)
    identf_d = din("identf", [128, 128])
    out_d = nc.dram_tensor("out", [S, D], F32, kind="ExternalOutput").ap()
    wgu_s = nc.dram_tensor("wgu_bf", [NPAIR, 128, 8 * 256], BF16, kind="Internal").ap()
    wdn_s = nc.dram_tensor("wdn_bf", [NPAIR, 128, D], BF16, kind="Internal").ap()

    es = contextlib.ExitStack()
    with es:
        sc = Sched(nc, es)

        def sb(name, shape, dt=F32):
            return es.enter_context(nc.sbuf_tensor("s_" + name, list(shape), dt))

        def psb(name, shape, dt=F32):
            return es.enter_context(nc.psum_tensor("p_" + name, list(shape), dt))

        arenaA = sb("arenaA", [128, 4096])
        arenaB = sb("arenaB", [128, 4096])
        hhT = sb("hhT", [128, NPAIR * T], BF16)
        actT = sb("actT", [128, 8 * T], BF16)
        win_bf = sb("win_bf", [128, 8 * 1792], BF16)
        wout_bf = sb("wout_bf", [128, 8 * D], BF16)
        qT = sb("qT", [128, 4 * T], BF16)
        kT = sb("kT", [128, 2 * (128 + T)], BF16)
        vsb = [sb("vsb%d" % i, [128, 2 * 65], BF16) for i in range(3)]
        mixT = sb("mixT", [128, 8 * T], BF16)
        mixb = sb("mixb", [128, D], BF16)
        p0 = sb("p0", [128, 8 * 256], BF16)
        pt = sb("pt", [128, 8 * 256], BF16)
        e_bf = sb("e_bf", [128, 8 * 256], BF16)
        gu = sb("gu", [128, 512])
        gvv = sb("gvv", [128, 512])
        sq = sb("sq", [128, 512])
        nbf = sb("nbf", [128, 512], BF16)
        t1 = sb("t1", [128, 512])
        gam = sb("gam", [128, 512])
        c2 = sb("c2", [128, 512])
        wsT_bf = sb("wsT_bf", [128, 8 * 128], BF16)
        xtok = [sb("xtok%d" % i, [128, D]) for i in range(2)]
        ostage = [sb("ostage%d" % i, [128, D]) for i in range(2)]
        sg = [sb("sg%d" % i, [128, 512]) for i in range(2)]
        slots = [sb("slot%d" % i, [128, 2048], BF16) for i in range(NSLOT)]
        lnt = {n: sb("lnt_" + n, [128, D]) for n in ("g1", "b1", "g2", "b2")}
        identb = sb("identb", [128, 128], BF16)
        identf = sb("identf", [128, 128])
        ones_row = sb("ones_row", [1, 128], BF16)
        brow = sb("brow", [1, 1152], BF16)
        ccol = sb("ccol", [128, 8])
        scol2 = sb("scol2", [128, 16])
        screp = xtok[0]
        badaT = sb("badaT", [128, 48])
        modc = sb("modc", [128, 32])
        bqk = sb("bqk", [128, 5])
        bq8 = sb("bq8", [128, 4])
        esink = sb("esink", [128, 8])
        gcol = sb("gcol", [128, 8])
        bsT = sb("bsT", [128, 8])
        rw = sb("rw", [128, 8])
        lnb_bc = gvv
        neghalf = sb("neghalf", [128, 8])
        st_s = sb("st_s", [128, 64])
        st_a = sb("st_a", [128, 32])
        st_g = sb("st_g", [128, 32])
        st_l = sb("st_l", [128, 8])
        attnf = sb("attnf", [128, 512])
        junk = sb("junk", [128, 512], BF16)
        bnst = sb("bnst", [128, 12])
        bnmv = sb("bnmv", [128, 2])

        bank = [psb("bank%d" % i, [128, 512]) for i in range(8)]

        B = {}

        def buf(name):
            if name not in B:
                B[name] = Buf(name)
            return B[name]

        bb = [buf("bank%d" % i) for i in range(8)]

        def v3(ap2d, a, b):
            return ap2d.rearrange("p (a b) -> p a b", a=a, b=b)

        def mm(out, lhsT, rhs, start, stop, reads, writes, inc, tp=None):
            def fn(e, out=out, lhsT=lhsT, rhs=rhs, start=start, stop=stop, tp=tp):
                if tp is None:
                    return e.matmul(out, lhsT=lhsT, rhs=rhs, start=start, stop=stop)
                return e.matmul(out, lhsT=lhsT, rhs=rhs, start=start, stop=stop, tile_position=tp)
            return sc.op("pe", fn, reads, writes, inc)

        def tr(out, in_, ident, reads, writes, inc):
            def fn(e, out=out, in_=in_, ident=ident):
                return e.transpose(out, in_, ident)
            return sc.op("pe", fn, reads, writes, inc)

        def act(out, in_, func, reads, writes, bias=None, scale=None, accum=None):
            def fn(e, out=out, in_=in_, func=func, bias=bias, scale=scale, accum=accum):
                kw = {}
                if bias is not None:
                    kw["bias"] = bias
                if scale is not None:
                    kw["scale"] = scale
                if accum is not None:
                    kw["accum_out"] = accum
                return e.activation(out=out, in_=in_, func=func, **kw)
            return sc.op("act", fn, reads, writes)

        def ts(en, out, in0, s1, s2, op0, op1, reads, writes):
            def fn(e, out=out, in0=in0, s1=s1, s2=s2, op0=op0, op1=op1):
                if op1 is None:
                    return e.tensor_scalar(out=out, in0=in0, scalar1=s1, scalar2=None, op0=op0)
                return e.tensor_scalar(out=out, in0=in0, scalar1=s1, scalar2=s2, op0=op0, op1=op1)
            return sc.op(en, fn, reads, writes)

        def tt(en, out, in0, in1, op, reads, writes):
            def fn(e, out=out, in0=in0, in1=in1, op=op):
                return e.tensor_tensor(out=out, in0=in0, in1=in1, op=op)
            return sc.op(en, fn, reads, writes)

        def stt(out, in0, scalar, in1, op0, op1, reads, writes):
            def fn(e, out=out, in0=in0, scalar=scalar, in1=in1, op0=op0, op1=op1):
                return e.scalar_tensor_tensor(out=out, in0=in0, scalar=scalar, in1=in1, op0=op0, op1=op1)
            return sc.op("dve", fn, reads, writes)

        def cp(en, out, in_, reads, writes):
            def fn(e, out=out, in_=in_, en=en):
                if en == "act":
                    return e.activation(out=out, in_=in_, func=AF.Identity)
                return e.tensor_copy(out=out, in_=in_)
            return sc.op(en, fn, reads, writes)

        def red(out, in_, reads, writes):
            def fn(e, out=out, in_=in_):
                return e.tensor_reduce(out=out, in_=in_, axis=AX.X, op=ALU.add)
            return sc.op("dve", fn, reads, writes)

        def memset(en, ap, val, writes):
            def fn(e, ap=ap, val=val):
                return e.memset(ap, val)
            return sc.op(en, fn, (), writes)

        def load(out, in_, writes, owner, reads=(), nonc=False):
            def fn(e, out=out, in_=in_, nonc=nonc):
                if nonc:
                    return e.dma_start(out=out, in_=in_, allow_slow_non_contiguous=True)
                return e.dma_start(out=out, in_=in_)
            return sc.dma(fn, reads, writes, owner=owner, kind="ld")

        def store(out, in_, reads, writes, owner, en="sp"):
            def fn(e, out=out, in_=in_):
                return e.dma_start(out=out, in_=in_)
            return sc.dma(fn, reads, writes, owner=owner, kind="st", en=en)

        def rsqrt_eps(dst, src, n, reads_writes):
            ts("dve", dst, src, EPS, None, ALU.add, None, reads_writes, reads_writes)
            tt("pool", dst, dst, neghalf[:, 0:n], ALU.pow, reads_writes + [buf("neghalf")], reads_writes)

        stg = [arenaA, arenaB]
        stgb = [buf("arenaA"), buf("arenaB")]
        small = buf("small")
        bhh = buf("hhT")

        load(ccol[:], ccol_d[:, :], [small], small)
        load(badaT[:], badaT_d[:, :], [small], small)
        load(bqk[:], bqk_d[:, :], [small], small)
        load(gcol[:], gcol_d[:, :], [small], small)
        load(bsT[:], bsT_d[:, :], [small], small)
        load(identb[:], identb_d[:, :], [small], small)
        load(identf[:], identf_d[:, :], [small], small)
        load(esink[:], sinks_d.partition_broadcast(128), [small], small)
        load(lnb_bc[:], lnb_d.partition_broadcast(128), [small], small)
        load(gam[:], lng_d.partition_broadcast(128), [small], small)
        browf = arenaA[0:1, 2304:3456]
        load(browf, bin_d[0:1, 640:1792], [stgb[0]], stgb[0])
        load(lnt["g1"][:], ln1g_d.partition_broadcast(128), [small], small)
        load(lnt["b1"][:], ln1b_d.partition_broadcast(128), [small], small)
        load(lnt["g2"][:], ln2g_d.partition_broadcast(128), [small], small)
        load(lnt["b2"][:], ln2b_d.partition_broadcast(128), [small], small)

        memset("dve", neghalf[:], -0.5, [buf("neghalf")])
        memset("dve", ones_row[:], 1.0, [small])
        for i in range(3):
            memset("dve", vsb[i][:], 1.0, [buf("vsb%d" % i)])
        memset("dve", p0[:], 0.0, [buf("p0")])
        memset("dve", kT[:], 0.0, [buf("kT")])
        cp("dve", brow[:], browf, [stgb[0]], [small])
        act(ccol[:], ccol[:], AF.Silu, [small], [small])
        act(esink[:], esink[:], AF.Exp, [small], [small])
        ts("dve", bq8[:], bqk[:, 0:4], 0.125, None, ALU.mult, None, [small], [small])
        for kt in range(8):
            cp("dve", scol2[:, 2 * kt:2 * kt + 2], ccol[:, kt:kt + 1].to_broadcast([128, 2]), [small], [small])
            cp("dve", screp[:, kt * 128:(kt + 1) * 128], ccol[:, kt:kt + 1].to_broadcast([128, 128]), [small], [small])

        load(arenaA[:, 0:2048], biasm_d[:, :], [stgb[0]], stgb[0])
        load(arenaA[:, 2048:2304], mask_d[:, :], [stgb[0]], stgb[0])
        act(arenaA[:, 0:2048], arenaA[:, 0:2048], AF.Exp, [stgb[0]], [stgb[0]])
        tt("dve", v3(e_bf[:], 8, 256), v3(arenaA[:, 0:2048], 8, 256),
           arenaA[:, 2048:2304].unsqueeze(1).to_broadcast([128, 8, 256]), ALU.mult, [stgb[0]], [buf("e_bf")])

        load(arenaB[:, 0:1024], wsT_d[:, :], [stgb[1]], stgb[1])
        load(arenaB[:, 1024:1152], trilT_d[:, :], [stgb[1]], stgb[1])
        load(arenaB[:, 2048:3072], wsN_d[:, :], [stgb[1]], stgb[1])
        load(arenaB[:, 3072:3200], tril_d[:, :], [stgb[1]], stgb[1])
        tt("dve", v3(wsT_bf[:], 8, 128), v3(arenaB[:, 0:1024], 8, 128),
           arenaB[:, 1024:1152].unsqueeze(1).to_broadcast([128, 8, 128]), ALU.mult, [stgb[1]], [buf("wsT_bf")])
        tt("dve", v3(arenaB[:, 2048:3072], 8, 128), v3(arenaB[:, 2048:3072], 8, 128),
           arenaB[:, 3072:3200].unsqueeze(1).to_broadcast([128, 8, 128]), ALU.mult, [stgb[1]], [stgb[1]])
        red(rw[:], v3(arenaB[:, 2048:3072], 8, 128), [stgb[1]], [small])
        for g in range(8):
            ts("dve", c2[:, g * 64:(g + 1) * 64], lnb_bc[:, g * 64:(g + 1) * 64], rw[:, g:g + 1], bsT[:, g:g + 1],
               ALU.mult, ALU.add, [small], [buf("c2")])

        g1bc = ostage[1]
        g2bc = ostage[0]
        load(g1bc[:], bada_d[0:1, 2 * D:3 * D].partition_broadcast(128), [buf("ostage1")], buf("ostage1"))
        load(g2bc[:], bada_d[0:1, 5 * D:6 * D].partition_broadcast(128), [buf("ostage0")], buf("ostage0"))
        colmap = {0: 0, 1: 0, 2: 8, 3: 8, 6: 16, 7: 16, 8: 24, 9: 24}
        screp_bf = mixb
        scol2_bf = junk[:, 0:16]
        cp("dve", screp_bf[:], screp[:], [small], [small])
        cp("dve", scol2_bf, scol2[:], [small], [small])
        for ch in range(12):
            sl = ch % 2
            sv = v3(stg[sl][:], 8, 512)
            load(sv, wada_d[:, ch * 512:(ch + 1) * 512].rearrange("(k p) n -> p k n", p=128), [stgb[sl]], stgb[sl])
            wb_ = v3(hhT[:, sl * 4096:(sl + 1) * 4096], 8, 512)
            hb = buf("ada_s%d" % sl)
            cp("dve", wb_[:, 0:3, :], sv[:, 0:3, :], [stgb[sl]], [hb])
            cp("act", wb_[:, 3:6, :], sv[:, 3:6, :], [stgb[sl]], [hb])
            cp("pool", wb_[:, 6:8, :], sv[:, 6:8, :], [stgb[sl]], [hb])
            if ch in colmap:
                for ft in range(4):
                    for kt in range(8):
                        mm(bank[0][:, 2 * ft:2 * ft + 2], wb_[:, kt, ft * 128:(ft + 1) * 128], scol2_bf[:, 2 * kt:2 * kt + 2],
                           kt == 0, kt == 7, [hb, small], [bb[0]], kt == 7 and ft == 3)
                c0 = colmap[ch] + (ch % 2) * 4
                j0 = ch * 4
                tt("dve", modc[:, c0:c0 + 4], v3(bank[0][:, 0:8], 4, 2)[:, :, 0], badaT[:, j0:j0 + 4], ALU.add,
                   [bb[0], small], [small])
            else:
                half = ch % 2
                dst = g1bc if ch < 6 else g2bc
                dbuf = buf("ostage1") if ch < 6 else buf("ostage0")
                for kt in range(8):
                    mm(bank[1][:, :], screp_bf[:, kt * 128:(kt + 1) * 128], wb_[:, kt, :], kt == 0, kt == 7,
                       [hb, small], [bb[1]], kt == 7)
                tt("dve", dst[:, half * 512:(half + 1) * 512], dst[:, half * 512:(half + 1) * 512], bank[1][:, :], ALU.add,
                   [bb[1], dbuf], [dbuf])
        ts("dve", modc[:, 8:16], modc[:, 8:16], 1.0, None, ALU.add, None, [small], [small])
        ts("dve", modc[:, 24:32], modc[:, 24:32], 1.0, None, ALU.add, None, [small], [small])

        for kt in range(8):
            sl = kt % 2
            load(stg[sl][:, 0:1792], win_d[kt * 128:(kt + 1) * 128, :], [stgb[sl]], stgb[sl])
            dstq = win_bf[:, kt * 1792:kt * 1792 + 512].rearrange("p (j g d) -> p j g d", j=4, g=2, d=64)
            srcq = stg[sl][:, 0:512].rearrange("p (g j d) -> p j g d", g=2, j=4, d=64)
            cp("dve", dstq, srcq, [stgb[sl]], [buf("win_bf")])
            cp("pool", win_bf[:, kt * 1792 + 512:(kt + 1) * 1792], stg[sl][:, 512:1792], [stgb[sl]], [buf("win_bf")])
        for kt in range(8):
            sl = kt % 2
            load(stg[sl][:, 0:1024], wout_d[kt * 128:(kt + 1) * 128, :], [stgb[sl]], stgb[sl])
            stt(wout_bf[:, kt * D:(kt + 1) * D], stg[sl][:, 0:1024], gcol[:, kt:kt + 1], g1bc[:], ALU.mult, ALU.mult,
                [stgb[sl], small, buf("ostage1")], [buf("wout_bf")])
        sc.barrier(list(B.values()))
        wgu_v = wgu_s.rearrange("i p (k c) -> i p k c", k=8, c=256)
        bwgu = buf("wgu_s")
        fq = [(arenaA if q < 4 else arenaB)[:, (q % 4) * 1024:(q % 4 + 1) * 1024] for q in range(8)]
        fqb = [buf("stq%d" % q) for q in range(8)]
        cast_eng = ("dve", "act", "pool")
        cnt = 0
        for kt in range(8):
            for half in range(2):
                for (c0, ncol) in ((0, 1024), (1024, 1024), (2048, 768)):
                    q = cnt % 8
                    o = cnt % 6
                    hs = hhT[:, o * 1024:o * 1024 + ncol]
                    hb = buf("hh_s%d" % o)
                    load(fq[q][:, 0:ncol], wgu_d[kt * 128:(kt + 1) * 128, half * DFF + c0:half * DFF + c0 + ncol], [fqb[q]], fqb[q])
                    cp(cast_eng[cnt % 3], hs, fq[q][:, 0:ncol], [fqb[q]], [hb])
                    i0, n = c0 // 128, ncol // 128
                    store(wgu_v[i0:i0 + n, :, kt, half * 128:(half + 1) * 128].rearrange("i p c -> p i c"),
                          hs.rearrange("p (i c) -> p i c", i=n, c=128), [hb], [bwgu], hb)
                    cnt += 1
        bwdn = buf("wdn_s")
        for kt in range(NPAIR):
            q = (cnt + kt) % 8
            o = kt % 4
            hs = hhT[:, 6144 + o * 1024:6144 + (o + 1) * 1024]
            hb = buf("hd_s%d" % o)
            load(fq[q], wdn_d[kt * 128:(kt + 1) * 128, :], [fqb[q]], fqb[q])
            tt("dve" if kt % 2 else "pool", hs, fq[q], g2bc[:], ALU.mult, [fqb[q], buf("ostage0")], [hb])
            store(wdn_s[kt], hs, [hb], [bwdn], hb)

        sc.barrier(list(B.values()))
        bxT, bx1 = stgb[0], stgb[1]
        bactA, bactB = buf("actT_a"), buf("actT_b")
        bqT, bkT, bmixT, bmixb = buf("qT"), buf("kT"), buf("mixT"), buf("mixb")
        bp0, bpt, battnf = buf("p0"), buf("pt"), buf("attnf")
        bgu, bgvv, bsq, bnbf, bt1 = buf("gu"), buf("gvv"), buf("sq"), buf("nbf"), buf("t1")
        bjunk = buf("junk")
        bsta, bstg, bstl = buf("st_a"), buf("st_g"), buf("st_l")
        bxtok = [buf("xtok0"), buf("xtok1")]
        bost = [buf("ostage0"), buf("ostage1")]
        bsg = [buf("sg0"), buf("sg1")]
        bslot = [buf("slot%d" % i) for i in range(NSLOT)]
        bvs = [buf("vsb%d" % i) for i in range(3)]
        bwin, bwout, be = buf("win_bf"), buf("wout_bf"), buf("e_bf")
        bc2, bws = buf("c2"), buf("wsT_bf")
        hT3 = v3(actT[:], 8, T)
        qT3 = v3(qT[:], 4, T)
        mixT3 = v3(mixT[:], 8, T)
        hh3 = v3(hhT[:], NPAIR, T)
        win3 = v3(win_bf[:], 8, 1792)
        wout3 = v3(wout_bf[:], 8, D)
        x13 = v3(arenaB[:], TB, D)
        xT3 = v3(arenaA[:], 8, T)
        p03 = v3(p0[:], 8, 256)
        pt3 = v3(pt[:], 8, 256)
        slot_i = [0]
        out_toks = []
        pend_st = []

        def flush_stores(force=False):
            keep = []
            for it in pend_st:
                it[0] -= 1
                if force or it[0] <= 0:
                    out_toks.append(it[1]())
                else:
                    keep.append(it)
            pend_st[:] = keep

        def next_slot():
            i = slot_i[0] % NSLOT
            slot_i[0] += 1
            flush_stores()
            return i

        def bc_last(ap2, n, m):
            return ap2.unsqueeze(2).to_broadcast([128, n, m])

        def pool_rsqrt(dst, src, n, b_):
            ts("dve", dst, src, EPS, None, ALU.add, None, [b_], [b_])
            tt("pool", dst, dst, neghalf[:, 0:n], ALU.pow, [b_, buf("neghalf")], [b_])

        def layer_norm_block(src, srcbuf, gt, bt, dst, dstbuf):
            for h in range(2):
                def fn(e, h=h):
                    return e.bn_stats(out=bnst[:, 6 * h:6 * h + 6], in_=src[:, h * 512:(h + 1) * 512])
                sc.op("dve", fn, [srcbuf], [bstl])

            def fn2(e):
                return e.bn_aggr(out=bnmv[:], in_=bnst[:])
            sc.op("dve", fn2, [bstl], [bstl])
            pool_rsqrt(st_l[:, 0:1], bnmv[:, 1:2], 1, bstl)
            ts("dve", st_l[:, 1:2], bnmv[:, 0:1], -1.0, None, ALU.mult, None, [bstl], [bstl])
            stt(src, src, st_l[:, 1:2], gt[:], ALU.add, ALU.mult, [srcbuf, bstl, small], [srcbuf])
            stt(dst, src, st_l[:, 0:1], bt[:], ALU.mult, ALU.add, [srcbuf, bstl, small], [dstbuf])

        def stage_a1(ti, b):
            t0 = ti * T
            gb = ti * TB + b
            tk = slice(b * 128, (b + 1) * 128)
            vcur = gb % 3
            xs = gb % 2
            load(xtok[xs][:], x_d[t0 + b * 128:t0 + (b + 1) * 128, :], [bxtok[xs]], bxtok[xs])
            for (c0, ncol, bk_, boff) in ((640, 128, 5, 0), (1280, 512, 1, 640), (768, 512, 0, 128)):
                o = bank[bk_][:, 256:384] if ncol == 128 else bank[bk_][:, :]
                for kt in range(8):
                    mm(o, hT3[:, kt, tk], win3[:, kt, c0:c0 + ncol], kt == 0, False, [bwin, bactA, bactB], [bb[bk_]], False)
                mm(o, ones_row[0:1, :], brow[0:1, boff:boff + ncol], False, True, [small], [bb[bk_]], True)
            cp("dve", v3(vsb[vcur][:], 2, 65)[:, :, 0:64], v3(bank[5][:, 256:384], 2, 64), [bb[5]], [bvs[vcur]])
            act(gvv[:], bank[1][:, :], AF.Gelu_apprx_tanh, [bb[1]], [bgvv])
            act(gu[:], bank[0][:, :], AF.Gelu_apprx_tanh, [bb[0]], [bgu])
            hv = ([0] if gb > 0 else []) + [1]
            for j in range(4):
                bx_, by_ = 2 + 2 * (j % 2), 3 + 2 * (j % 2)
                for hf in hv:
                    kc = slice(b * 128 + hf * 128, b * 128 + hf * 128 + 128)
                    last = hf == 1
                    kc2 = slice(KW + b * 128 + hf * 128, KW + b * 128 + hf * 128 + 128)
                    mm(bank[bx_][:, hf * 128:(hf + 1) * 128], kT[:, kc], qT3[:, j, tk], True, True,
                       [bkT, bqT], [bb[bx_]], last)
                    mm(bank[by_][:, hf * 128:(hf + 1) * 128], kT[:, kc2], qT3[:, j, tk], True, True,
                       [bkT, bqT], [bb[by_]], last)
                lo = hv[0] * 128
                act(p03[:, j, lo:256], bank[bx_][:, lo:256], AF.Exp, [bb[bx_]], [bp0])
                act(p03[:, 4 + j, lo:256], bank[by_][:, lo:256], AF.Exp, [bb[by_]], [bp0])

        def stage_a2(ti, b):
            gb = ti * TB + b
            tk = slice(b * 128, (b + 1) * 128)
            vcur, vprev = gb % 3, (gb - 1) % 3
            gvv3 = v3(gvv[:], 8, 64)
            red(st_g[:, 0:8], gvv3, [bgvv], [bstg])
            tt("pool", sq[:], gvv[:], gvv[:], ALU.mult, [bgvv], [bsq])
            red(st_g[:, 8:16], v3(sq[:], 8, 64), [bsq], [bstg])
            ts("dve", st_g[:, 0:8], st_g[:, 0:8], 1.0 / 64, None, ALU.mult, None, [bstg], [bstg])
            tt("dve", st_g[:, 16:24], st_g[:, 0:8], st_g[:, 0:8], ALU.mult, [bstg], [bstg])
            stt(st_g[:, 8:16], st_g[:, 8:16], 1.0 / 64, st_g[:, 16:24], ALU.mult, ALU.subtract, [bstg], [bstg])
            pool_rsqrt(st_g[:, 8:16], st_g[:, 8:16], 8, bstg)
            tt("dve", pt[:], p0[:], e_bf[:], ALU.mult, [bp0, be], [bpt])
            for h in range(8):
                g = h // 4
                o = bank[6 + g][:, (h % 4) * 65:(h % 4) * 65 + 65]
                if gb > 0:
                    mm(o, pt3[:, h, 0:128], v3(vsb[vprev][:], 2, 65)[:, g, :], True, False, [bpt, bvs[vprev]], [bb[6 + g]], False)
                mm(o, pt3[:, h, 128:256], v3(vsb[vcur][:], 2, 65)[:, g, :], gb == 0, True, [bpt, bvs[vcur]], [bb[6 + g]],
                   h % 4 == 3)
            tt("dve", gvv3, gvv3, bc_last(st_g[:, 0:8], 8, 64), ALU.subtract, [bgvv, bstg], [bgvv])
            tt("dve", v3(nbf[:], 8, 64), gvv3, bc_last(st_g[:, 8:16], 8, 64), ALU.mult, [bgvv, bstg], [bnbf])
            for g in range(8):
                mm(bank[2][:, g * 64:(g + 1) * 64], v3(wsT_bf[:], 8, 128)[:, g, :], nbf[:, g * 64:(g + 1) * 64], True, True,
                   [bws, bnbf], [bb[2]], g == 7)
            for g in range(2):
                tt("dve", st_a[:, 4 * g:4 * g + 4], v3(bank[6 + g][:, 0:260], 4, 65)[:, :, 64], esink[:, 4 * g:4 * g + 4],
                   ALU.add, [bb[6 + g], small], [bsta])

            def fnr(e):
                return e.reciprocal(out=st_a[:, 8:16], in_=st_a[:, 0:8])
            sc.op("dve", fnr, [bsta], [bsta])
            for g in range(2):
                tt("dve", v3(attnf[:, g * 256:(g + 1) * 256], 4, 64), v3(bank[6 + g][:, 0:260], 4, 65)[:, :, 0:64],
                   bc_last(st_a[:, 8 + 4 * g:12 + 4 * g], 4, 64), ALU.mult, [bb[6 + g], bsta], [battnf])
            act(junk[:], attnf[:], AF.Square, [battnf], [bjunk, bsta], accum=st_a[:, 16:17])
            ts("dve", st_a[:, 17:18], st_a[:, 16:17], 1.0 / 512, EPS, ALU.mult, ALU.add, [bsta], [bsta])
            tt("pool", st_a[:, 17:18], st_a[:, 17:18], neghalf[:, 0:1], ALU.pow, [bsta, buf("neghalf")], [bsta])
            act(mixb[:, 0:512], attnf[:], AF.Identity, [battnf, bsta], [bmixb], scale=st_a[:, 17:18])
            tt("dve", t1[:], bank[2][:, :], gam[:], ALU.mult, [bb[2], small], [bt1])
            tt("dve", t1[:], t1[:], c2[:], ALU.add, [bt1, bc2], [bt1])
            tt("dve", t1[:], t1[:], gu[:], ALU.mult, [bt1, bgu], [bt1])
            act(junk[:], t1[:], AF.Square, [bt1], [bjunk, bstg], accum=st_g[:, 24:25])
            ts("dve", st_g[:, 25:26], st_g[:, 24:25], 1.0 / 512, EPS, ALU.mult, ALU.add, [bstg], [bstg])
            tt("pool", st_g[:, 25:26], st_g[:, 25:26], neghalf[:, 0:1], ALU.pow, [bstg, buf("neghalf")], [bstg])
            act(mixb[:, 512:1024], t1[:], AF.Identity, [bt1, bstg], [bmixb], scale=st_g[:, 25:26])
            trp = bank[3][:, :].bitcast(BF16)
            for kt in range(8):
                tr(trp[:, kt * 128:(kt + 1) * 128], mixb[:, kt * 128:(kt + 1) * 128], identb[:], [bmixb, small], [bb[3]], kt == 7)
            cp("act", mixT3[:, :, tk], v3(trp, 8, 128), [bb[3]], [bmixT])
            for half in range(2):
                for kt in range(8):
                    mm(bank[6 + half][:, :], mixT3[:, kt, tk], wout3[:, kt, half * 512:(half + 1) * 512], kt == 0, kt == 7,
                       [bmixT, bwout], [bb[6 + half]], kt == 7)

        def stage_b(ti, b):
            gb = ti * TB + b
            tk = slice(b * 128, (b + 1) * 128)
            xs = gb % 2
            for half in range(2):
                hs_ = slice(half * 512, (half + 1) * 512)
                stt(xtok[xs][:, hs_], xtok[xs][:, hs_], ALPHA, bank[6 + half][:, :], ALU.mult, ALU.add,
                    [bxtok[xs], bb[6 + half]], [bxtok[xs]])
            layer_norm_block(xtok[xs][:], bxtok[xs], lnt["g1"], lnt["b1"], x13[:, b, :], bx1)
            for kt in range(8):
                bk_ = 4 + kt // 4
                tr(bank[bk_][:, (kt % 4) * 128:(kt % 4 + 1) * 128], x13[:, b, kt * 128:(kt + 1) * 128], identf[:],
                   [bx1, small], [bb[bk_]], kt % 4 == 3)
            for kt in range(8):
                bk_ = 4 + kt // 4
                src = bank[bk_][:, (kt % 4) * 128:(kt % 4 + 1) * 128]
                if kt >= 4:
                    ts("dve", hT3[:, kt, tk], src, modc[:, 24 + kt:25 + kt], modc[:, 16 + kt:17 + kt], ALU.mult, ALU.add,
                       [bb[bk_], small], [bactB])
                else:
                    act(hT3[:, kt, tk], src, AF.Identity, [bb[bk_], small], [bactA], bias=modc[:, 16 + kt:17 + kt],
                        scale=modc[:, 24 + kt:25 + kt])

        load(xT3, xT_d[:, 0:T].rearrange("(k p) t -> p k t", p=128), [bxT], bxT)
        for ti in range(NT):
            t0 = ti * T
            for kt in range(8):
                if kt >= 4:
                    ts("dve", hT3[:, kt, :], xT3[:, kt, :], modc[:, 8 + kt:9 + kt], modc[:, kt:kt + 1], ALU.mult, ALU.add,
                       [bxT, small], [bactB])
                else:
                    act(hT3[:, kt, :], xT3[:, kt, :], AF.Identity, [bxT, small], [bactA], bias=modc[:, kt:kt + 1],
                        scale=modc[:, 8 + kt:9 + kt])
            if ti + 1 < NT:
                load(xT3, xT_d[:, t0 + T:t0 + 2 * T].rearrange("(k p) t -> p k t", p=128), [bxT], bxT)
            for m in range(5):
                bk_ = m % 2
                for kt in range(8):
                    mm(bank[bk_][:, :], win3[:, kt, m * 128:(m + 1) * 128], hT3[:, kt, :], kt == 0, kt == 7,
                       [bwin, bactA, bactB], [bb[bk_]], kt == 7)
                if m < 4:
                    act(qT3[:, m, :], bank[bk_][:, :], AF.Identity, [bb[bk_], small], [bqT], bias=bq8[:, m:m + 1], scale=0.125)
                else:
                    act(kT[0:64, 128:128 + T], bank[bk_][0:64, :], AF.Identity, [bb[bk_], small], [bkT], bias=bqk[0:64, 4:5])
                    act(kT[64:128, KW + 128:KW + 128 + T], bank[bk_][64:128, :], AF.Identity, [bb[bk_], small], [bkT],
                        bias=bqk[64:128, 4:5])
            for s_ in range(TB + 1):
                if s_ < TB:
                    stage_a1(ti, s_)
                if s_ >= 1:
                    stage_b(ti, s_ - 1)
                if s_ < TB:
                    stage_a2(ti, s_)
            cp("dve", kT[:, 0:128], kT[:, T:T + 128], [bkT], [bkT])
            cp("dve", kT[:, KW:KW + 128], kT[:, KW + T:KW + T + 128], [bkT], [bkT])

            for p in range(NPAIR):
                si = next_slot()
                load(slots[si][:], wgu_s[p], [bslot[si]], bslot[si], reads=[bwgu])
                sv = v3(slots[si][:], 8, 256)
                bg, bu = (0, 1) if p % 2 == 0 else (2, 3)
                for kt in range(8):
                    mm(bank[bg][:, :], sv[:, kt, 0:128], hT3[:, kt, :], kt == 0, kt == 7, [bslot[si], bactA, bactB], [bb[bg]], kt == 7)
                for kt in range(8):
                    mm(bank[bu][:, :], sv[:, kt, 128:256], hT3[:, kt, :], kt == 0, kt == 7, [bslot[si], bactA, bactB], [bb[bu]], kt == 7)
                act(sg[p % 2][:], bank[bg][:, :], AF.Silu, [bb[bg]], [bsg[p % 2]])
                tt("dve", hh3[:, p, :], bank[bu][:, :], sg[p % 2][:], ALU.mult, [bb[bu], bsg[p % 2]], [bhh])
            for bp in range(2):
                for kc in range(NPAIR // 2):
                    si = next_slot()
                    load(v3(slots[si][:], 2, D), wdn_s[2 * kc:2 * kc + 2].rearrange("k p n -> p k n"), [bslot[si]], bslot[si],
                         reads=[bwdn])
                    sv = v3(slots[si][:], 2, D)
                    for kk in range(2):
                        kt = 2 * kc + kk
                        for bl in range(2):
                            b = 2 * bp + bl
                            for half in range(2):
                                bk_ = 4 + bl * 2 + half
                                mm(bank[bk_][:, :], hh3[:, kt, b * 128:(b + 1) * 128], sv[:, kk, half * 512:(half + 1) * 512],
                                   kt == 0, kt == NPAIR - 1, [bslot[si], bhh], [bb[bk_]],
                                   (kk == 1 and bl == 1 and half == 1))
                for bl in range(2):
                    b = 2 * bp + bl
                    gb = ti * TB + b
                    for half in range(2):
                        hs_ = slice(half * 512, (half + 1) * 512)
                        stt(x13[:, b, hs_], x13[:, b, hs_], ALPHA, bank[4 + bl * 2 + half][:, :], ALU.mult, ALU.add,
                            [bx1, bb[4 + bl * 2 + half]], [bx1])
                    os_ = gb % 2
                    layer_norm_block(x13[:, b, :], bx1, lnt["g2"], lnt["b2"], ostage[os_][:], bost[os_])
                    pend_st.append([5, (lambda r0=t0 + b * 128, os_=os_: store(out_d[r0:r0 + 128, :], ostage[os_][:], [bost[os_]],
                                                                           [buf("out_d")], bost[os_], en=STORE_ENG))])

        flush_stores(force=True)
        for d in (bost[0].st, bost[1].st):
            if d is not None:
                sc.wait_tok("sp", Tok(d[2], d[0], d[1]))

        with nc.allow_non_contiguous_dma("weight re-layout into bf16 scratch"), nc.Block() as block:
            @block.sync
            def _(e):
                sc.replay("sp", e)

            @block.tensor
            def _(e):
                sc.replay("pe", e)

            @block.scalar
            def _(e):
                sc.replay("act", e)

            @block.vector
            def _(e):
                sc.replay("dve", e)

            @block.gpsimd
            def _(e):
                sc.replay("pool", e)
    return nc


def _t5_bucket(dist):
    n = np.maximum(dist, 0)
    nf = np.maximum(n, 16).astype(np.float32)
    large = 16 + (np.log(nf / np.float32(16)) / np.float32(np.log(128 / 16)) * np.float32(16)).astype(np.int32)
    large = np.minimum(large, 31)
    return np.where(n < 16, n, large)


def _const_inputs():
    s = np.arange(128)[:, None]
    c = np.arange(256)[None, :]
    q = np.where(c < 128, c, c - 128)
    dist = np.where(c < 128, q + 128 - s, q - s)
    valid = (dist >= 0) & (dist < 128)
    bucket = _t5_bucket(dist)
    tril_ts = (np.arange(128)[None, :] <= np.arange(128)[:, None]).astype(np.float32)
    return bucket, valid.astype(np.float32), tril_ts


def make_in_maps(inputs, S):
    f = lambda a: np.ascontiguousarray(np.asarray(a, dtype=np.float32))
    x = f(inputs["x"])
    Bn = x.shape[0]
    bucket, valid, tril_ts = _const_inputs()
    rel_bias = f(inputs["rel_bias"])
    biasm = np.ascontiguousarray(rel_bias[bucket].transpose(0, 2, 1)).reshape(128, 8 * 256)
    b_in = f(inputs["b_in"])[0]
    bq = b_in[0:512].reshape(2, 4, 64).transpose(0, 2, 1).reshape(128, 4)
    bqk = np.ascontiguousarray(np.concatenate([bq, b_in[512:640].reshape(128, 1)], axis=1))
    w_s = f(inputs["gmlp_w_s"])[0]
    shared = {
        "w_ada": f(inputs["w_ada"])[0],
        "badaT": np.ascontiguousarray(f(inputs["b_ada"])[0].reshape(48, 128).T),
        "b_ada": f(inputs["b_ada"]).reshape(1, -1),
        "w_in": f(inputs["w_in"])[0],
        "b_in": f(inputs["b_in"]).reshape(1, -1),
        "bqk": bqk,
        "sinks": f(inputs["attn_sinks"]).reshape(1, 8),
        "biasm": biasm,
        "maskm": valid,
        "gmlp_ln_g": f(inputs["gmlp_ln_g"]).reshape(1, 512),
        "gmlp_ln_b": f(inputs["gmlp_ln_b"]).reshape(1, 512),
        "wsT": np.ascontiguousarray(w_s.transpose(2, 0, 1)).reshape(128, 1024),
        "wsN": np.ascontiguousarray(w_s.transpose(1, 0, 2)).reshape(128, 1024),
        "trilT": np.ascontiguousarray(tril_ts.T),
        "tril": tril_ts,
        "bsT": np.ascontiguousarray(f(inputs["gmlp_b_s"])[0].T),
        "gcol": np.ascontiguousarray(np.concatenate([f(inputs["attn_out_g"])[0], f(inputs["gmlp_out_g"])[0]]).reshape(8, 128).T),
        "w_out": f(inputs["w_out"])[0],
        "ln1_g": f(inputs["ln1_g"]).reshape(1, -1),
        "ln1_b": f(inputs["ln1_b"]).reshape(1, -1),
        "w_gate_up": f(inputs["w_gate_up"])[0],
        "w_down": f(inputs["w_down"])[0],
        "ln2_g": f(inputs["ln2_g"]).reshape(1, -1),
        "ln2_b": f(inputs["ln2_b"]).reshape(1, -1),
        "identb": np.eye(128, dtype=ml_dtypes.bfloat16),
        "identf": np.eye(128, dtype=np.float32),
    }
    c = f(inputs["c"])
    maps = []
    for b in range(Bn):
        m = dict(shared)
        m["x"] = np.ascontiguousarray(x[b, :S])
        m["xT"] = np.ascontiguousarray(x[b, :S].T)
        m["ccol"] = np.ascontiguousarray(c[b].reshape(8, 128).T)
        maps.append(m)
    return maps


_NC_CACHE = {}


def kernel(**inputs):
    x = np.asarray(inputs["x"])
    Bn, S, _ = x.shape
    if S not in _NC_CACHE:
        _NC_CACHE[S] = build_program(S)
    nc = _NC_CACHE[S]
    maps = make_in_maps(inputs, S)
    res = run_bass_kernel_spmd(nc, maps, core_ids=list(range(Bn)))
    return np.stack([np.asarray(r["out"], dtype=np.float32) for r in res.results], axis=0)
```
